# Optimizing a Trainium2 kernel written in Bass

```python
import jax, jax.numpy as jnp
from jax import lax
import numpy as np

D_MODEL = 4096
BATCH = 4
SEQ = 4096
DEPTH = 4

MIX_W = D_MODEL
ATT_W = MIX_W // 2
HEAD_DIM = 128
N_Q_HEADS = ATT_W // HEAD_DIM
N_KV_HEADS = N_Q_HEADS // 4
KV_W = N_KV_HEADS * HEAD_DIM
AXIS_DIM = HEAD_DIM // 2
ROPE_THETA = 10000.0
Q_BLOCK = 128
GRID_W = 64

REC_W = MIX_W - ATT_W
REC_HEADS = 16
REC_BLK = REC_W // REC_HEADS
REC_CONV = 4
RG_C = 8.0

IN_W = ATT_W + 2 * KV_W + 2 * REC_W

N_MEM = 256
X_HEADS = 4
X_HEAD_DIM = 256
X_W = X_HEADS * X_HEAD_DIM

D_FF = 3 * D_MODEL
FFN_CONV = 3
EPS = 1e-6

kernel_name = "bidir_hymba_rglru_gqa_axialrope_convffn"


def rmsnorm(x, g):
    xf = x.astype(jnp.float32)
    y = xf * lax.rsqrt(jnp.mean(xf * xf, axis=-1, keepdims=True) + EPS)
    return (y * g.astype(jnp.float32)).astype(x.dtype)


def dwconv_centred(x, w, b):
    K = w.shape[0]
    left = (K - 1) // 2
    right = K - 1 - left
    S = x.shape[1]
    xp = jnp.pad(x, ((0, 0), (left, right), (0, 0)))
    y = xp[:, 0:S] * w[0]
    for k in range(1, K):
        y = y + xp[:, k:k + S] * w[k]
    return y + b


def axial_rope_tables(S):
    rows = S // GRID_W
    row = jnp.repeat(jnp.arange(rows, dtype=jnp.float32), GRID_W)
    col = jnp.tile(jnp.arange(GRID_W, dtype=jnp.float32), rows)
    inv = ROPE_THETA ** (-jnp.arange(0, AXIS_DIM, 2, dtype=jnp.float32) / AXIS_DIM)
    ang = jnp.stack([row[:, None] * inv, col[:, None] * inv], axis=1)
    return jnp.cos(ang), jnp.sin(ang)


def apply_axial_rope(x, cos, sin):
    B, S, H, hd = x.shape
    xf = x.astype(jnp.float32).reshape(B, S, H, 2, 2, AXIS_DIM // 2)
    x1 = xf[..., 0, :]
    x2 = xf[..., 1, :]
    c = cos[None, :, None]
    s = sin[None, :, None]
    out = jnp.stack([x1 * c - x2 * s, x2 * c + x1 * s], axis=-2)
    return out.reshape(B, S, H, hd).astype(x.dtype)


def blocked_gqa(q, k, v):
    B, S, H, hd = q.shape
    KV = k.shape[2]
    G = H // KV
    NB = S // Q_BLOCK
    qb = q.reshape(B, NB, Q_BLOCK, KV, G, hd).transpose(1, 0, 2, 3, 4, 5)
    scale = hd ** -0.5

    def one_block(qblk):
        s = jnp.einsum('bqkgd,bskd->bkgqs', qblk, k,
                       preferred_element_type=jnp.float32) * scale
        p = jax.nn.softmax(s, axis=-1).astype(v.dtype)
        return jnp.einsum('bkgqs,bskd->bqkgd', p, v)

    o = lax.map(one_block, qb)
    return o.transpose(1, 0, 2, 3, 4, 5).reshape(B, S, H * hd)


def rglru_direction(xc, w_gates, b_gates, lam, reverse):
    B, S, W = xc.shape
    xh = xc.reshape(B, S, REC_HEADS, REC_BLK)
    gates = jnp.einsum('bshi,ghij->gbshj', xh, w_gates).reshape(2, B, S, W) + b_gates[:, None, None, :]
    r = jax.nn.sigmoid(gates[0].astype(jnp.float32))
    i = jax.nn.sigmoid(gates[1].astype(jnp.float32))
    log_a = -RG_C * r * jax.nn.softplus(-lam.astype(jnp.float32))
    a = jnp.exp(log_a)
    b = jnp.sqrt(-jnp.expm1(2.0 * log_a)) * (i * xc.astype(jnp.float32))
    if reverse:
        a = jnp.flip(a, axis=1)
        b = jnp.flip(b, axis=1)

    def combine(lhs, rhs):
        a1, b1 = lhs
        a2, b2 = rhs
        return a1 * a2, a2 * b1 + b2

    _, h = lax.associative_scan(combine, (a, b), axis=1)
    if reverse:
        h = jnp.flip(h, axis=1)
    return h


def hybrid_mixer(h, w_in, q_g, k_g, conv_w, conv_b, gate_w, gate_b, lam, group_g, w_out, cos, sin):
    B, S, _ = h.shape
    z = h @ w_in
    q, k, v, xr, gr = jnp.split(
        z, [ATT_W, ATT_W + KV_W, ATT_W + 2 * KV_W, ATT_W + 2 * KV_W + REC_W], axis=-1)
    q = q.reshape(B, S, N_Q_HEADS, HEAD_DIM)
    k = k.reshape(B, S, N_KV_HEADS, HEAD_DIM)
    v = v.reshape(B, S, N_KV_HEADS, HEAD_DIM)
    q = apply_axial_rope(rmsnorm(q, q_g), cos, sin)
    k = apply_axial_rope(rmsnorm(k, k_g), cos, sin)
    att = blocked_gqa(q, k, v)
    xc = dwconv_centred(xr, conv_w, conv_b)
    hs = (rglru_direction(xc, gate_w[0], gate_b[0], lam[0], False)
          + rglru_direction(xc, gate_w[1], gate_b[1], lam[1], True))
    rec = (jax.nn.gelu(gr.astype(jnp.float32), approximate=True) * hs).astype(h.dtype)
    y = jnp.concatenate([rmsnorm(att, group_g[:ATT_W]), rmsnorm(rec, group_g[ATT_W:])], axis=-1)
    return y @ w_out


def memory_cross_attn(h, mem_n, w_cq, w_ckv, w_co):
    B, S, _ = h.shape
    M = mem_n.shape[1]
    q = (h @ w_cq).reshape(B, S, X_HEADS, X_HEAD_DIM)
    kv = (mem_n @ w_ckv).reshape(B, M, 2, X_HEADS, X_HEAD_DIM)
    k = kv[:, :, 0]
    v = kv[:, :, 1]
    s = jnp.einsum('bqhd,bmhd->bhqm', q, k, preferred_element_type=jnp.float32) * (X_HEAD_DIM ** -0.5)
    p = jax.nn.softmax(s, axis=-1).astype(v.dtype)
    o = jnp.einsum('bhqm,bmhd->bqhd', p, v).reshape(B, S, X_W)
    return o @ w_co


def conv_gated_mlp(h, w_up, conv_w, conv_b, w_down):
    u = dwconv_centred(h @ w_up, conv_w, conv_b)
    g, val = jnp.split(u, 2, axis=-1)
    return (jax.nn.gelu(g, approximate=True) * val) @ w_down


def _dense(k, shape, fan_in):
    return jax.random.normal(k, shape, jnp.float32) * (fan_in ** -0.5)


def _gain(k, shape):
    return 1.0 + 0.02 * jax.random.normal(k, shape, jnp.float32)


def _bias(k, shape):
    return 0.02 * jax.random.normal(k, shape, jnp.float32)


def setup_inputs(seed: int = 0) -> dict:
    key = jax.random.key(seed)
    ks = jax.random.split(key, 24)
    x = jax.random.normal(ks[0], (BATCH, SEQ, D_MODEL), jnp.float32)
    mem = jax.random.normal(ks[1], (BATCH, N_MEM, D_MODEL), jnp.float32)
    u = jax.random.uniform(ks[11], (DEPTH, 2, REC_W), jnp.float32, 0.81, 0.998)
    a = jnp.sqrt(u) ** (1.0 / RG_C)
    rg_lambda = jnp.log(a) - jnp.log1p(-a)
    return {
        "x": x,
        "mem": mem,
        "mem_norm": _gain(ks[2], (D_MODEL,)),
        "norm_mix": _gain(ks[3], (DEPTH, D_MODEL)),
        "w_in": _dense(ks[4], (DEPTH, D_MODEL, IN_W), D_MODEL),
        "q_norm": _gain(ks[5], (DEPTH, HEAD_DIM)),
        "k_norm": _gain(ks[6], (DEPTH, HEAD_DIM)),
        "rg_conv_w": _dense(ks[7], (DEPTH, REC_CONV, REC_W), REC_CONV),
        "rg_conv_b": _bias(ks[8], (DEPTH, REC_W)),
        "rg_gate_w": _dense(ks[9], (DEPTH, 2, 2, REC_HEADS, REC_BLK, REC_BLK), REC_BLK),
        "rg_gate_b": _bias(ks[10], (DEPTH, 2, 2, REC_W)),
        "rg_lambda": rg_lambda,
        "group_norm": _gain(ks[12], (DEPTH, MIX_W)),
        "w_out": _dense(ks[13], (DEPTH, MIX_W, D_MODEL), MIX_W),
        "norm_cross": _gain(ks[14], (DEPTH, D_MODEL)),
        "w_cq": _dense(ks[15], (DEPTH, D_MODEL, X_W), D_MODEL),
        "w_ckv": _dense(ks[16], (DEPTH, D_MODEL, 2 * X_W), D_MODEL),
        "w_co": _dense(ks[17], (DEPTH, X_W, D_MODEL), X_W),
        "norm_ffn": _gain(ks[18], (DEPTH, D_MODEL)),
        "w_up": _dense(ks[19], (DEPTH, D_MODEL, 2 * D_FF), D_MODEL),
        "ffn_conv_w": _dense(ks[20], (DEPTH, FFN_CONV, 2 * D_FF), FFN_CONV),
        "ffn_conv_b": _bias(ks[21], (DEPTH, 2 * D_FF)),
        "w_down": _dense(ks[22], (DEPTH, D_FF, D_MODEL), D_FF),
        "final_norm": _gain(ks[23], (D_MODEL,)),
    }


def reference(x, mem, mem_norm, norm_mix, w_in, q_norm, k_norm, rg_conv_w, rg_conv_b,
              rg_gate_w, rg_gate_b, rg_lambda, group_norm, w_out, norm_cross, w_cq, w_ckv,
              w_co, norm_ffn, w_up, ffn_conv_w, ffn_conv_b, w_down, final_norm):
    S = x.shape[1]
    cos, sin = axial_rope_tables(S)
    mem_n = rmsnorm(mem, mem_norm)
    h = x
    for l in range(DEPTH):
        h = h + hybrid_mixer(rmsnorm(h, norm_mix[l]), w_in[l], q_norm[l], k_norm[l],
                             rg_conv_w[l], rg_conv_b[l], rg_gate_w[l], rg_gate_b[l],
                             rg_lambda[l], group_norm[l], w_out[l], cos, sin)
        h = h + memory_cross_attn(rmsnorm(h, norm_cross[l]), mem_n, w_cq[l], w_ckv[l], w_co[l])
        h = h + conv_gated_mlp(rmsnorm(h, norm_ffn[l]), w_up[l], ffn_conv_w[l], ffn_conv_b[l], w_down[l])
    return rmsnorm(h, final_norm)
```

```python
import numpy as np
import ml_dtypes
from contextlib import ExitStack
import concourse.bass as bass
import concourse.mybir as mybir
from concourse.bass_utils import run_bass_kernel_spmd

F32 = mybir.dt.float32
BF16 = mybir.dt.bfloat16
AF = mybir.ActivationFunctionType
ALU = mybir.AluOpType
EPS = 1e-6
NCORES = 8


def make_cfg(D=4096, T=4096, L=4, GRID_W=64, NMEM=256, B=4):
    c = dict(D=D, T=T, L=L, GRID_W=GRID_W, NMEM=NMEM, B=B)
    c["DC"] = D // 128
    att_w = D // 2
    c["NQ"] = att_w // 128
    c["NKV"] = c["NQ"] // 4
    c["RC"] = (D - att_w) // 128
    c["FIN"] = c["NQ"] + 2 * c["NKV"] + 2 * c["RC"]
    c["XC"] = 8
    c["DFF"] = 3 * D
    c["FC"] = c["DFF"] // 128
    c["TT"] = 512
    c["TH"] = T // 2
    c["NT"] = (T // 2) // 512
    DC, RC, FC = c["DC"], c["RC"], c["FC"]
    off = {}
    o = 0
    for name, n in [("nm", DC), ("nc", DC), ("nf", DC), ("gn", DC), ("qn", 1), ("kn", 1), ("cw", 4 * RC), ("cb", RC),
                    ("gb", 4 * RC), ("lam", 2 * RC), ("fw", 6 * FC), ("fb", 2 * FC)]:
        off[name] = o
        o += n
    c["poff"] = off
    c["NPL"] = o
    c["W"] = {
        "w_in": (c["FIN"] * 128, DC * 128),
        "gate": (RC * 128, 4 * 128),
        "w_out": (DC * 128, DC * 128),
        "w_ckv": (2 * c["XC"] * 128, DC * 128),
        "w_cq": (c["XC"] * 128, DC * 128),
        "w_co": (DC * 128, c["XC"] * 128),
    }
    c["SPLIT"] = {"w_up": 4, "w_down": 2}
    for q in range(4):
        c["W"]["w_up%d" % q] = (2 * FC * 128 // 4, DC * 128)
    for q in range(2):
        c["W"]["w_down%d" % q] = (DC * 128 // 2, FC * 128)
    c["NCHK"] = {}
    for name, (rows, cols) in c["W"].items():
        nb = rows * cols * 2 // NCORES
        n = 1
        while nb // n > 8 * 1024 * 1024:
            n *= 2
        assert n == 1 or (rows // 128 // NCORES) % n == 0
        c["NCHK"][name] = n
    return c


class Op:
    __slots__ = ("eng", "fn", "deps", "kind", "signal", "ms", "dsem", "dval", "dprev")


class Sched:
    ENG = ["pe", "act", "dve", "pool", "sp"]
    ND = 8

    def __init__(self, nc, stack):
        self.nc = nc
        self.ops = {e: [] for e in self.ENG}
        self.all = []
        self.lw = {}
        self.rd = {}
        self.psem = {e: stack.enter_context(nc.semaphore("p_" + e)) for e in ("pe", "act", "dve")}
        self.dsem = {q: [stack.enter_context(nc.semaphore("d_%s%d" % (q, i))) for i in range(self.ND)]
                     for q in ("sp", "pool")}
        self.ccsem = [stack.enter_context(nc.semaphore("ccsem%d" % i)) for i in range(2)]
        self.dmas = {"sp": [], "pool": []}
        self.ncc = [0, 0]
        self.lastcc = [None, None]
        self.pending = {e: set() for e in self.ENG}

    def add(self, eng, fn, reads=(), writes=(), kind="c", ccq=0):
        op = Op()
        op.eng, op.fn, op.kind, op.signal, op.ms = eng, fn, kind, False, 0
        deps = set(self.pending[eng])
        self.pending[eng] = set()
        for k in reads:
            w = self.lw.get(k)
            if w is not None:
                deps.add(w)
        for k in writes:
            for r in self.rd.get(k, ()):
                deps.add(r)
            w = self.lw.get(k)
            if w is not None:
                deps.add(w)
        inorder = eng in ("pe", "act", "dve")
        for k in reads:
            lst = self.rd.setdefault(k, [])
            if inorder:
                for i_, r_ in enumerate(lst):
                    if r_.eng == eng:
                        del lst[i_]
                        break
            lst.append(op)
        for k in writes:
            self.lw[k] = op
            self.rd[k] = []
        op.deps = deps
        op.dprev = None
        if kind == "dma":
            lst = self.dmas[eng]
            i = len(lst)
            op.dsem = self.dsem[eng][i % self.ND]
            op.dval = 16 * (i // self.ND + 1)
            if i >= self.ND:
                op.dprev = lst[i - self.ND]
            lst.append(op)
        elif kind == "cc":
            self.ncc[ccq] += 1
            op.dsem = self.ccsem[ccq]
            op.dval = self.ncc[ccq]
            self.lastcc[ccq] = op
        self.ops[eng].append(op)
        self.all.append(op)
        return op

    def dma(self, q, out, in_, reads=(), writes=(), slow=False):
        if slow:
            return self.add(q, lambda e: e.dma_start(out=out, in_=in_, allow_slow_non_contiguous=True), reads, writes, kind="dma")
        return self.add(q, lambda e: e.dma_start(out=out, in_=in_), reads, writes, kind="dma")

    def barrier(self):
        fence = set()
        for e in ("pe", "act", "dve"):
            if self.ops[e]:
                fence.add(self.ops[e][-1])
        for q in ("sp", "pool"):
            for op in self.dmas[q][-self.ND:]:
                fence.add(op)
        for lc in self.lastcc:
            if lc is not None:
                fence.add(lc)
        for e in self.ENG:
            self.pending[e] |= fence
        keep_lw = {k: v for k, v in self.lw.items() if k[0] in ("wfull", "wsh")}
        keep_rd = {k: v for k, v in self.rd.items() if k[0] in ("wfull", "wsh")}
        self.lw, self.rd = keep_lw, keep_rd

    def finalize(self):
        for op in self.all:
            for d in op.deps:
                if d.kind == "c" and not (d.eng == "pe" and op.eng == "pe"):
                    d.signal = True
        for e in ("pe", "act", "dve"):
            n = 0
            for op in self.ops[e]:
                if op.signal:
                    n += 1
                    op.ms = n

    def emit(self, e, h):
        known = {}
        for op in self.ops[e]:
            waits = {}
            for d in op.deps:
                if d.kind == "c":
                    if d.eng == "pe" and e == "pe":
                        continue
                    sem, val = self.psem[d.eng], d.ms
                else:
                    sem, val = d.dsem, d.dval
                k = id(sem)
                if k not in waits or waits[k][1] < val:
                    waits[k] = (sem, val)
            if op.dprev is not None:
                k = id(op.dprev.dsem)
                if k not in waits or waits[k][1] < op.dprev.dval:
                    waits[k] = (op.dprev.dsem, op.dprev.dval)
            for k, (sem, val) in waits.items():
                if known.get(k, 0) < val:
                    h.wait_ge(sem, val)
                    known[k] = val
            inst = op.fn(h)
            if op.kind == "dma":
                inst.then_inc(op.dsem, 16)
            elif op.kind == "cc":
                inst.then_inc(op.dsem)
            elif op.signal:
                inst.then_inc(self.psem[e], 1)


def build_program(cfg):
    D, T, L, DC, NQ, NKV, RC, FIN, XC, FC, TT, NT, NMEM = (cfg[k] for k in
        ("D", "T", "L", "DC", "NQ", "NKV", "RC", "FIN", "XC", "FC", "TT", "NT", "NMEM"))
    NPL, poff = cfg["NPL"], cfg["poff"]
    KCT = T // 128
    TH = cfg["TH"]
    KH = KCT // 2
    PAIRS = [[2 * i, 2 * i + 1] for i in range(NCORES // 2)]
    nc = bass.Bass("TRN2", target_bir_lowering=False)

    xT = nc.dram_tensor("xT", [DC, 128, TH], F32, kind="ExternalInput")
    msk = nc.dram_tensor("msk", [128, 2], F32, kind="ExternalInput")
    memT = nc.dram_tensor("memT", [DC, 128, NMEM], F32, kind="ExternalInput")
    ppl = nc.dram_tensor("ppl", [L, 128, NPL], F32, kind="ExternalInput")
    ppg = nc.dram_tensor("ppg", [128, 2 * DC], F32, kind="ExternalInput")
    cosT = nc.dram_tensor("cosT", [128, TH], F32, kind="ExternalInput")
    sinT = nc.dram_tensor("sinT", [128, TH], F32, kind="ExternalInput")
    cmat = nc.dram_tensor("cmat", [128, 3 * 128], BF16, kind="ExternalInput")
    outT = nc.dram_tensor("outT", [DC, 128, TH], F32, kind="ExternalOutput")
    wext, wsh, wfull = {}, {}, {}
    for name, (rows, cols) in cfg["W"].items():
        n = rows * cols // NCORES
        wext[name] = nc.dram_tensor("e_" + name, [L, 128, n // 128], F32, kind="ExternalInput")
        wsh[name] = [nc.dram_tensor("s_%s%d" % (name, l), [rows // NCORES, cols], BF16) for l in range(L)]
        wfull[name] = [nc.dram_tensor("f_%s%d" % (name, l), [rows, cols], BF16) for l in range(L)]
    H = [nc.dram_tensor("H%d" % i, [DC, 128, TH + 2], F32) for i in range(2)]
    qTd = nc.dram_tensor("qTd", [NQ, 128, TH], BF16)
    VR = 4 if KH % 4 == 0 else 1
    KR = KH // VR
    exK = nc.dram_tensor("exK", [NKV, 128 * TH], BF16)
    gK = nc.dram_tensor("gK", [NKV, 2, 128 * TH], BF16)
    exV = nc.dram_tensor("exV", [VR, KR * 128 * NKV * 128], BF16)
    gV = nc.dram_tensor("gV", [VR, 2, KR * 128 * NKV * 128], BF16)
    exX = nc.dram_tensor("exX", [2 * RC, 128 * TH], F32)
    gX = nc.dram_tensor("gX", [2 * RC, 2, 128 * TH], F32)
    NH_ = DC * 128 * 2
    exH = nc.dram_tensor("exH", [1, NH_], F32)
    gH = nc.dram_tensor("gH", [2, NH_], F32)
    kTd = exK.ap().rearrange("g (p t) -> g p t", p=128)
    Vd = exV.ap().rearrange("r (k p d) -> (r k) p d", p=128, d=NKV * 128)
    xrTd = exX.ap()[0:RC].rearrange("c (p t) -> c p t", p=128)
    grTd = exX.ap()[RC:2 * RC].rearrange("c (p t) -> c p t", p=128)
    gk = [gK.ap()[:, h, :].rearrange("g (p t) -> g p t", p=128) for h in range(2)]
    gv = [[gV.ap()[r, h, :].rearrange("(k p d) -> k p d", p=128, d=NKV * 128) for r in range(VR)] for h in range(2)]
    gxr = [gX.ap()[0:RC, h, :].rearrange("c (p t) -> c p t", p=128) for h in range(2)]
    ggr = [gX.ap()[RC:2 * RC, h, :].rearrange("c (p t) -> c p t", p=128) for h in range(2)]
    exHv = exH.ap()[0, :].rearrange("(c p two) -> c p two", p=128, two=2)
    gHv = [gH.ap()[h, :].rearrange("(c p two) -> c p two", p=128, two=2) for h in range(2)]
    attTd = nc.dram_tensor("attTd", [NQ, 128, TH], BF16)
    recTd = nc.dram_tensor("recTd", [RC, 128, TH], BF16)
    actd = nc.dram_tensor("actd", [FC, 128, TH], BF16)
    memnTd = nc.dram_tensor("memnTd", [DC, 128, NMEM], BF16)

    stack = ExitStack()
    with stack:
        S = Sched(nc, stack)
        ARENA = 180 * 1024
        arena = stack.enter_context(nc.sbuf_tensor("arena", [128, ARENA // 2], BF16))
        consts = stack.enter_context(nc.sbuf_tensor("consts", [128, 3 * 128], BF16))
        ppg_sb = stack.enter_context(nc.sbuf_tensor("ppg_sb", [128, 2 * DC], F32))
        msk_sb = stack.enter_context(nc.sbuf_tensor("msk_sb", [128, 2], F32))
        ppl_sb = stack.enter_context(nc.sbuf_tensor("ppl_sb", [128, NPL], F32))
        cc_sb = stack.enter_context(nc.sbuf_tensor("cc_sb", [128, 4 * RC], F32))
        tmp_sb = stack.enter_context(nc.sbuf_tensor("tmp_sb", [128, 4 * RC], F32))
        ps = [stack.enter_context(nc.psum_tensor("ps%d" % i, [128, 512], F32)) for i in range(8)]
        ones = consts[:, 0:128]
        ident = consts[:, 128:256]
        rotT = consts[:, 256:384]

        class Arena:
            def __init__(self):
                self.off = 0

            def reset(self):
                self.off = 0

            def alloc(self, shape, dt):
                esz = 4 if dt == F32 else 2
                n = 1
                for s in shape:
                    n *= s
                nb = (n * esz + 63) // 64 * 64
                assert self.off + nb <= ARENA, ("arena overflow", self.off, nb)
                v = arena[:, self.off // 2:(self.off + nb) // 2]
                self.off += nb
                if dt == F32:
                    v = v.bitcast(F32)
                v = v[:, 0:n]
                if len(shape) == 2:
                    v = v.rearrange("p (a b) -> p a b", a=shape[0])
                elif len(shape) == 3:
                    v = v.rearrange("p (a b c) -> p a b c", a=shape[0], b=shape[1])
                return v

        A = Arena()
        uid = [0]

        def key(name):
            uid[0] += 1
            return (name, uid[0])

        def PS(i):
            return ("ps", i)

        def pair_gather(src_ap, dst_ap, rk, wk):
            S.add("pool", lambda g: g.collective_compute("AllGather", ALU.bypass, replica_groups=PAIRS,
                                                         ins=[src_ap.opt()], outs=[dst_ap.opt()]),
                  reads=[rk], writes=[wk], kind="cc", ccq=1)

        def mm(out, lhsT, rhs, start, stop, reads, writes):
            return S.add("pe", lambda e: e.matmul(out, lhsT, rhs, start=start, stop=stop), reads, writes)

        def actf(out, in_, func, reads, writes, scale=1.0, bias=0.0):
            return S.add("act", lambda e: e.activation(out=out, in_=in_, func=func, scale=scale, bias=bias), reads, writes)

        def tt(out, a, b, op, reads, writes):
            return S.add("dve", lambda e: e.tensor_tensor(out=out, in0=a, in1=b, op=op), reads, writes)

        def stt(out, in0, scalar, in1, op0, op1, reads, writes):
            return S.add("dve", lambda e: e.scalar_tensor_tensor(out=out, in0=in0, scalar=scalar, in1=in1, op0=op0, op1=op1),
                         reads, writes)

        def ts(out, in0, s1, s2, op0, op1, reads, writes):
            return S.add("dve", lambda e: e.tensor_scalar(out=out, in0=in0, scalar1=s1, scalar2=s2, op0=op0, op1=op1), reads, writes)

        def recip(out, in_, reads, writes):
            return S.add("dve", lambda e: e.reciprocal(out=out, in_=in_), reads, writes)

        def pcol(name, i=0):
            o = poff[name] + i
            return ppl_sb[:, o:o + 1]

        S.dma("sp", consts[:], cmat[:, :], writes=[("consts",)])
        S.dma("sp", ppg_sb[:], ppg[:, :], writes=[("ppg",)])
        S.dma("sp", msk_sb[:], msk[:, :], writes=[("msk",)])
        A.reset()
        PIECE = 4096
        cin = [A.alloc([PIECE], F32) for _ in range(3)]
        cout = [A.alloc([PIECE], BF16) for _ in range(3)]
        order = ["w_in", "gate", "w_out", "w_ckv", "w_cq", "w_co", "w_up0", "w_up1", "w_up2", "w_up3", "w_down0", "w_down1"]
        pieces = []
        for l in range(L):
            for name in order:
                rows, cols = cfg["W"][name]
                m = rows * cols // NCORES // 128
                dst = wsh[name][l].ap().rearrange("r c -> (r c)").rearrange("(p m) -> p m", p=128)
                cs = list(range(0, m, PIECE))
                for c0 in cs:
                    pieces.append((name, l, dst, c0, min(PIECE, m - c0), c0 == cs[-1]))

        def cast_load(i):
            name, l, dst, c0, w, last = pieces[i]
            S.dma("sp", cin[i % 3][:, 0:w], wext[name][l, :, c0:c0 + w], writes=[("cin", i % 3)])
        cast_load(0)
        for i, (name, l, dst, c0, w, last) in enumerate(pieces):
            if i + 1 < len(pieces):
                cast_load(i + 1)
            b = i % 3
            if i % 2 == 0:
                S.add("dve", (lambda o, i_: lambda e: e.tensor_copy(out=o, in_=i_))(cout[b][:, 0:w], cin[b][:, 0:w]),
                      [("cin", b)], [("cout", b)])
            else:
                actf(cout[b][:, 0:w], cin[b][:, 0:w], AF.Copy, [("cin", b)], [("cout", b)])
            S.dma("sp", dst[:, c0:c0 + w], cout[b][:, 0:w], reads=[("cout", b)], writes=[("wsh", name, l)])
            if last:
                S.add("pool", (lambda i_, o_: lambda g: g.collective_compute(
                    "AllGather", ALU.bypass, replica_groups=[list(range(NCORES))], ins=[i_], outs=[o_]))(
                    wsh[name][l].ap().opt(), wfull[name][l].ap().opt()),
                    reads=[("wsh", name, l)], writes=[("wfull", name, l)], kind="cc")
        for c in range(DC):
            S.dma("sp", H[0][c, :, 1:TH + 1], xT[c], writes=[("h", 0, c)])
        S.barrier()

        def wview(name, l):
            return wfull[name][l].ap().rearrange("(j p) (c f) -> j p c f", p=128, f=128)

        class WRing:
            def __init__(self, n, kc):
                self.bufs = [A.alloc([kc, 128], BF16) for _ in range(n)]
                self.n = n
                self.i = 0
                self.kc = kc

            def load(self, name, l, j, c0, c1):
                if name in cfg["SPLIT"]:
                    nsp = cfg["SPLIT"][name]
                    jb = cfg["W"][name + "0"][0] // 128
                    name, j = name + str(j // jb), j % jb
                b = self.i % self.n
                self.i += 1
                S.dma("sp", self.bufs[b][:, 0:c1 - c0, :], wview(name, l)[j, :, c0:c1, :],
                      reads=[("wfull", name, l)], writes=[("wb", id(self), b)])
                return self.bufs[b], ("wb", id(self), b)

        def col_pieces(n):
            if n <= 512:
                return [(0, n)]
            h = n // 2
            return [(0, h), (h, n)]

        def norm_tile(Hsrc, hkeys, t0, t1, gains, hn, off, nrm, out_f32=None):
            n = t1 - t0
            G = nrm["G"]
            pcs = col_pieces(n)
            ngr = DC // G
            for gi in range(ngr):
                b = nrm["i"] % 2
                nrm["i"] += 1
                raw = nrm["raw"][b]
                sq = nrm["sq"][b]
                S.dma("sp", raw[:, :, 0:n], Hsrc.ap()[gi * G:(gi + 1) * G, :, t0:t1].rearrange("c p t -> p c t"),
                      reads=hkeys, writes=[("raw", b)])
                actf(sq[:, :, 0:n], raw[:, :, 0:n], AF.Square, [("raw", b)], [("sq", b)])
                for c in range(G):
                    for pi_, (a0, a1) in enumerate(pcs):
                        first = (gi == 0 and c == 0)
                        last = (gi == ngr - 1 and c == G - 1)
                        mm(ps[nrm["ps"][pi_]][:, 0:a1 - a0], ones, sq[:, c, a0:a1], first, last,
                           [("sq", b), ("consts",)], [PS(nrm["ps"][pi_])])
            rstd = nrm["rstd"]
            for pi_, (a0, a1) in enumerate(pcs):
                actf(rstd[:, a0:a1], ps[nrm["ps"][pi_]][:, 0:a1 - a0], AF.Sqrt, [PS(nrm["ps"][pi_])], [("rstd",)],
                     scale=1.0 / D, bias=EPS)
            recip(rstd[:, 0:n], rstd[:, 0:n], [("rstd",)], [("rstd",)])
            for gi in range(ngr):
                b = nrm["i"] % 2
                nrm["i"] += 1
                raw = nrm["raw"][b]
                S.dma("sp", raw[:, :, 0:n], Hsrc.ap()[gi * G:(gi + 1) * G, :, t0:t1].rearrange("c p t -> p c t"),
                      reads=hkeys, writes=[("raw", b)])
                for c in range(G):
                    cg = gi * G + c
                    if out_f32 is None:
                        stt(hn[:, cg, off:off + n], raw[:, c, 0:n], gains(cg), rstd[:, 0:n], ALU.mult, ALU.mult,
                            [("raw", b), ("rstd",), ("ppl",), ("ppg",)], [("hn", cg)])
                    else:
                        out_f32(gi, c, b, raw, rstd)

        def make_nrm(G, W, psbanks):
            return dict(G=G, i=0, ps=psbanks, raw=[A.alloc([G, W], F32) for _ in range(2)],
                        sq=[A.alloc([G, W], BF16) for _ in range(2)], rstd=A.alloc([1, W], F32)[:, 0, :])

        A.reset()
        nrm = make_nrm(4, 512, [6, 7])
        hn = A.alloc([DC, 512], BF16)
        norm_tile(memT, [], 0, NMEM, lambda cg: ppg_sb[:, cg:cg + 1], hn, 0, nrm)
        S.dma("pool", memnTd.ap().rearrange("c p t -> p c t"), hn[:, :, 0:NMEM], reads=[("hn", c) for c in range(DC)],
              writes=[("memn",)])
        S.barrier()

        for l in range(L):
            Hc = H[l % 2]
            Hn = H[(l + 1) % 2]
            hb = l % 2
            S.dma("sp", ppl_sb[:], ppl[l], writes=[("ppl",)])
            lam = ppl_sb[:, poff["lam"]:poff["lam"] + 2 * RC]
            t1_, t2_ = tmp_sb[:, 0:2 * RC], tmp_sb[:, 2 * RC:4 * RC]
            ts(t2_, lam, -1.0, None, ALU.mult, ALU.bypass, [("ppl",)], [("tmp2",)])
            tt(t1_, lam, t2_, ALU.max, [("ppl",), ("tmp2",)], [("tmp1",)])
            actf(t1_, t1_, AF.Exp, [("tmp1",)], [("tmp1",)], scale=-1.0)
            actf(t1_, t1_, AF.Ln, [("tmp1",)], [("tmp1",)], scale=1.0, bias=1.0)
            ts(t2_, t2_, 0.0, None, ALU.max, ALU.bypass, [("tmp2",)], [("tmp2",)])
            tt(t1_, t1_, t2_, ALU.add, [("tmp1",), ("tmp2",)], [("tmp1",)])
            ts(cc_sb[:, 0:2 * RC], t1_, -8.0, None, ALU.mult, ALU.bypass, [("tmp1",)], [("cc",)])
            ts(cc_sb[:, 2 * RC:4 * RC], t1_, -16.0, None, ALU.mult, ALU.bypass, [("tmp1",)], [("cc",)])
            S.barrier()

            A.reset()
            nrm = make_nrm(4, 512, [6, 7])
            hn = A.alloc([DC, 512], BF16)
            wr = WRing(3, DC)
            ctab = [A.alloc([1, 512], F32)[:, 0, :] for _ in range(2)]
            stab = [A.alloc([1, 512], F32)[:, 0, :] for _ in range(2)]
            sqh = [A.alloc([1, 512], BF16)[:, 0, :] for _ in range(2)]
            ub = [A.alloc([1, 512], BF16)[:, 0, :] for _ in range(2)]
            rs = [A.alloc([1, 512], F32)[:, 0, :] for _ in range(2)]
            t1b = [A.alloc([1, 512], F32)[:, 0, :] for _ in range(2)]
            t2b = [A.alloc([1, 512], F32)[:, 0, :] for _ in range(2)]
            qn = [A.alloc([1, 512], BF16)[:, 0, :] for _ in range(2)]
            vT = [A.alloc([1, 512], BF16)[:, 0, :] for _ in range(2)]
            vtok = [A.alloc([4, NKV * 128], BF16) for _ in range(2)]
            ev = [A.alloc([1, 512], F32)[:, 0, :] for _ in range(3)]
            psb = ps[5][:].bitcast(BF16)
            ei = 0
            hi = 0
            for tI in range(NT):
                t0 = tI * TT
                norm_tile(Hc, [], t0 + 1, t0 + 1 + TT, lambda cg: pcol("nm", cg), hn, 0, nrm)
                tb = tI % 2
                S.dma("sp", ctab[tb], cosT[:, t0:t0 + TT], writes=[("ctab", tb)])
                S.dma("sp", stab[tb], sinT[:, t0:t0 + TT], writes=[("stab", tb)])
                for j in range(FIN):
                    wbuf, wk = wr.load("w_in", l, j, 0, DC)
                    zb = j % 2
                    for c in range(DC):
                        mm(ps[zb][:, :], wbuf[:, c, :], hn[:, c, :], c == 0, c == DC - 1, [wk, ("hn", c)], [PS(zb)])
                    if j < NQ + NKV:
                        isq = j < NQ
                        b = hi % 2
                        hi += 1
                        gcol = pcol("qn") if isq else pcol("kn")
                        actf(sqh[b], ps[zb][:, :], AF.Square, [PS(zb)], [("sqh", b)])
                        actf(ub[b], ps[zb][:, :], AF.Identity, [PS(zb), ("ppl",)], [("ub", b)], scale=gcol)
                        mm(ps[3][:, :], ones, sqh[b], True, True, [("sqh", b)], [PS(3)])
                        mm(ps[4][:, :], rotT, ub[b], True, True, [("ub", b)], [PS(4)])
                        actf(rs[b], ps[3][:, :], AF.Sqrt, [PS(3)], [("rs", b)], scale=1.0 / 128, bias=EPS)
                        recip(rs[b], rs[b], [("rs", b)], [("rs", b)])
                        tt(t1b[b], ub[b], ctab[tb], ALU.mult, [("ub", b), ("ctab", tb)], [("t1b", b)])
                        tt(t2b[b], ps[4][:, :], stab[tb], ALU.mult, [PS(4), ("stab", tb)], [("t2b", b)])
                        tt(t1b[b], t1b[b], t2b[b], ALU.add, [("t1b", b), ("t2b", b)], [("t1b", b)])
                        tt(qn[b], t1b[b], rs[b], ALU.mult, [("t1b", b), ("rs", b)], [("qn", b)])
                        dst = qTd[j, :, t0:t0 + TT] if isq else kTd[j - NQ, :, t0:t0 + TT]
                        S.dma("pool", dst, qn[b], reads=[("qn", b)], writes=[key("qk")])
                    elif j < NQ + 2 * NKV:
                        g = j - NQ - NKV
                        b = g % 2
                        vb = tI % 2
                        actf(vT[b], ps[zb][:, :], AF.Copy, [PS(zb)], [("vT", b)])
                        for s in range(4):
                            S.add("pe", (lambda o, i: lambda e: e.transpose(out=o, in_=i, identity=ident))(
                                psb[:, s * 128:(s + 1) * 128], vT[b][:, s * 128:(s + 1) * 128]),
                                [("vT", b), ("consts",)], [PS(5)])
                        S.add("dve", (lambda o, i: lambda e: e.tensor_copy(out=o, in_=i))(
                            vtok[vb][:, :, g * 128:(g + 1) * 128], psb[:, 0:512].rearrange("p (s d) -> p s d", s=4)),
                            [PS(5)], [("vtok", vb)])
                        if g == NKV - 1:
                            S.dma("pool", Vd[tI * 4:(tI + 1) * 4].rearrange("s p d -> p s d"), vtok[vb],
                                  reads=[("vtok", vb)], writes=[key("V")])
                    else:
                        r = j - NQ - 2 * NKV
                        b = ei % 3
                        ei += 1
                        actf(ev[b], ps[zb][:, :], AF.Copy, [PS(zb)], [("ev", b)])
                        dst = xrTd[r, :, t0:t0 + TT] if r < RC else grTd[r - RC, :, t0:t0 + TT]
                        S.dma("pool", dst, ev[b], reads=[("ev", b)], writes=[key("xg")])
            S.barrier()
            for g_ in range(NKV):
                pair_gather(exK.ap()[g_:g_ + 1, :], gK.ap()[g_], ("exK",), key("gK"))
            for r_ in range(VR):
                pair_gather(exV.ap()[r_:r_ + 1, :], gV.ap()[r_], ("exV",), key("gV"))
            for c_ in range(2 * RC):
                pair_gather(exX.ap()[c_:c_ + 1, :], gX.ap()[c_], ("exX",), key("gX"))
            S.barrier()

            A.reset()
            xp = A.alloc([1, T + 4], F32)[:, 0, :]
            xc = A.alloc([1, T], F32)[:, 0, :]
            xcb = A.alloc([1, T], BF16)[:, 0, :]
            Rb = A.alloc([1, T], F32)[:, 0, :]
            Ib = A.alloc([1, T], F32)[:, 0, :]
            Ab = A.alloc([1, T], F32)[:, 0, :]
            Hd = [A.alloc([1, T], F32)[:, 0, :] for _ in range(2)]
            grb = A.alloc([1, T], F32)[:, 0, :]
            recb = A.alloc([1, TH], BF16)[:, 0, :]
            gw = [A.alloc([4, 128], BF16) for _ in range(2)]
            gview = wfull["gate"][l].ap().rearrange("(h p) (g f) -> h p g f", p=128, f=128)
            S.add("dve", lambda e: e.memset(xp[:, 0:1], 0.0), [], [("xp",)])
            S.add("dve", lambda e: e.memset(xp[:, T + 1:T + 4], 0.0), [], [("xp",)])
            for j in range(RC):
                for h in range(2):
                    S.dma("sp", xp[:, 1 + h * TH:1 + (h + 1) * TH], gxr[h][j], writes=[("xp",)])
                    S.dma("sp", grb[:, h * TH:(h + 1) * TH], ggr[h][j], writes=[("grb",)])
                gb_ = j % 2
                S.dma("sp", gw[gb_], gview[j], reads=[("wfull", "gate", l)], writes=[("gw", gb_)])
                actf(xc, xp[:, 0:T], AF.Identity, [("xp",), ("ppl",)], [("xc",)], scale=pcol("cw", 0 * RC + j), bias=pcol("cb", j))
                for k in range(1, 4):
                    stt(xc, xp[:, k:k + T], pcol("cw", k * RC + j), xc, ALU.mult, ALU.add, [("xp",), ("xc",), ("ppl",)], [("xc",)])
                actf(xcb, xc, AF.Copy, [("xc",)], [("xcb",)])
                for d in range(2):
                    for g in range(2):
                        dstb = Rb if g == 0 else Ib
                        dk = ("Rb",) if g == 0 else ("Ib",)
                        for s in range(T // 512):
                            pb = s % 4
                            mm(ps[pb][:, :], gw[gb_][:, d * 2 + g, :], xcb[:, s * 512:(s + 1) * 512], True, True,
                               [("gw", gb_), ("xcb",)], [PS(pb)])
                            actf(dstb[:, s * 512:(s + 1) * 512], ps[pb][:, :], AF.Sigmoid, [PS(pb), ("ppl",)], [dk],
                                 bias=pcol("gb", (d * 2 + g) * RC + j))
                    actf(Ab, Rb, AF.Exp, [("Rb",), ("cc",)], [("Ab",)], scale=cc_sb[:, d * RC + j:d * RC + j + 1])
                    actf(Rb, Rb, AF.Exp, [("Rb",), ("cc",)], [("Rb",)], scale=cc_sb[:, 2 * RC + d * RC + j:2 * RC + d * RC + j + 1])
                    actf(Rb, Rb, AF.Sqrt, [("Rb",)], [("Rb",)], scale=-1.0, bias=1.0)
                    tt(Ib, Ib, xc, ALU.mult, [("Ib",), ("xc",)], [("Ib",)])
                    tt(Ib, Ib, Rb, ALU.mult, [("Ib",), ("Rb",)], [("Ib",)])
                    if d == 0:
                        S.add("dve", lambda e: e.tensor_tensor_scan(out=Hd[0], data0=Ab, data1=Ib, initial=0.0, op0=ALU.mult, op1=ALU.add),
                              [("Ab",), ("Ib",)], [("Hd", 0)])
                    else:
                        S.add("dve", lambda e: e.tensor_tensor_scan(out=Hd[1][:, ::-1], data0=Ab[:, ::-1], data1=Ib[:, ::-1], initial=0.0,
                                                                      op0=ALU.mult, op1=ALU.add), [("Ab",), ("Ib",)], [("Hd", 1)])
                tt(Hd[0], Hd[0], Hd[1], ALU.add, [("Hd", 0), ("Hd", 1)], [("Hd", 0)])
                actf(grb, grb, AF.Gelu_apprx_tanh, [("grb",)], [("grb",)])
                tt(Hd[1], grb, Hd[0], ALU.mult, [("grb",), ("Hd", 0)], [("Hd", 1)])
                ts(recb, Hd[1][:, 0:TH], msk_sb[:, 1:2], None, ALU.mult, ALU.bypass, [("Hd", 1)], [("recb",)])
                stt(recb, Hd[1][:, TH:T], msk_sb[:, 0:1], recb, ALU.mult, ALU.add, [("Hd", 1), ("recb",)], [("recb",)])
                S.dma("pool", recTd[j], recb, reads=[("recb",)], writes=[key("rec")])
            S.barrier()

            A.reset()
            KT = A.alloc([NKV, T], BF16)
            Vs = A.alloc([KCT, NKV * 128], BF16)
            qt = [A.alloc([1, 512], BF16)[:, 0, :] for _ in range(2)]
            pt = [A.alloc([1, 512], BF16)[:, 0, :] for _ in range(3)]
            rl = [A.alloc([1, 512], F32)[:, 0, :] for _ in range(2)]
            ao = [A.alloc([1, 512], BF16)[:, 0, :] for _ in range(2)]
            for h in range(2):
                S.dma("sp", KT[:, :, h * TH:(h + 1) * TH], gk[h].rearrange("g p t -> p g t"), writes=[("KT",)])
                for r_ in range(VR):
                    S.dma("sp", Vs[:, h * KH + r_ * KR:h * KH + (r_ + 1) * KR, :], gv[h][r_].rearrange("k p d -> p k d"),
                          writes=[("Vs",)])
            it = 0
            sc = 128.0 ** -0.5
            for tI in range(NT):
                t0 = tI * TT
                for h in range(NQ):
                    g = h // 4
                    b = it % 2
                    it += 1
                    pso, psl = 3 + b, 5 + b
                    S.dma("sp", qt[b], qTd[h, :, t0:t0 + TT], writes=[("qt", b)])

                    def s_mm(kc):
                        mm(ps[kc % 3][:, :], KT[:, g, kc * 128:(kc + 1) * 128], qt[b], True, True, [("KT",), ("qt", b)], [PS(kc % 3)])
                    s_mm(0)
                    if KCT > 1:
                        s_mm(1)
                    for kc in range(KCT):
                        pb = kc % 3
                        actf(pt[pb], ps[pb][:, :], AF.Exp, [PS(pb)], [("pt", pb)], scale=sc)
                        if kc + 2 < KCT:
                            s_mm(kc + 2)
                        mm(ps[pso][:, :], Vs[:, kc, g * 128:(g + 1) * 128], pt[pb], kc == 0, kc == KCT - 1,
                           [("pt", pb), ("Vs",)], [PS(pso)])
                        mm(ps[psl][:, :], ones, pt[pb], kc == 0, kc == KCT - 1, [("pt", pb)], [PS(psl)])
                    recip(rl[b], ps[psl][:, :], [PS(psl)], [("rl", b)])
                    tt(ao[b], ps[pso][:, :], rl[b], ALU.mult, [PS(pso), ("rl", b)], [("ao", b)])
                    S.dma("pool", attTd[h, :, t0:t0 + TT], ao[b], reads=[("ao", b)], writes=[key("att")])
            S.barrier()

            A.reset()
            yraw = A.alloc([DC, 512], BF16)
            y = A.alloc([DC, 512], BF16)
            sqc = [A.alloc([1, 512], BF16)[:, 0, :] for _ in range(2)]
            rsa = A.alloc([1, 512], F32)[:, 0, :]
            rsr = A.alloc([1, 512], F32)[:, 0, :]
            wr = WRing(3, DC)
            hr = [A.alloc([1, 512], F32)[:, 0, :] for _ in range(3)]
            ta = [A.alloc([1, 512], F32)[:, 0, :] for _ in range(2)]
            tb_ = [A.alloc([1, 512], F32)[:, 0, :] for _ in range(2)]
            ri = 0
            for tI in range(NT):
                t0 = tI * TT
                S.dma("sp", yraw[:, 0:NQ, :], attTd.ap()[:, :, t0:t0 + TT].rearrange("c p t -> p c t"), writes=[("yraw", 0)])
                S.dma("sp", yraw[:, NQ:DC, :], recTd.ap()[:, :, t0:t0 + TT].rearrange("c p t -> p c t"), writes=[("yraw", 1)])
                for c in range(DC):
                    half = 0 if c < NQ else 1
                    b = c % 2
                    actf(sqc[b], yraw[:, c, :], AF.Square, [("yraw", half)], [("sqc", b)])
                    cl = c if half == 0 else c - NQ
                    nlast = (NQ if half == 0 else RC) - 1
                    mm(ps[6 + half][:, :], ones, sqc[b], cl == 0, cl == nlast, [("sqc", b)], [PS(6 + half)])
                    ts(y[:, c, :], yraw[:, c, :], pcol("gn", c), None, ALU.mult, ALU.bypass, [("yraw", half), ("ppl",)], [("y", c)])
                actf(rsa, ps[6][:, :], AF.Sqrt, [PS(6)], [("rsa",)], scale=1.0 / (NQ * 128), bias=EPS)
                actf(rsr, ps[7][:, :], AF.Sqrt, [PS(7)], [("rsr",)], scale=1.0 / (RC * 128), bias=EPS)
                recip(rsa, rsa, [("rsa",)], [("rsa",)])
                recip(rsr, rsr, [("rsr",)], [("rsr",)])
                for j in range(DC):
                    wbuf, wk = wr.load("w_out", l, j, 0, DC)
                    pa, pr = (j % 2) * 2, (j % 2) * 2 + 1
                    for c in range(NQ):
                        mm(ps[pa][:, :], wbuf[:, c, :], y[:, c, :], c == 0, c == NQ - 1, [wk, ("y", c)], [PS(pa)])
                    for c in range(NQ, DC):
                        mm(ps[pr][:, :], wbuf[:, c, :], y[:, c, :], c == NQ, c == DC - 1, [wk, ("y", c)], [PS(pr)])
                    b = ri % 3
                    b2 = ri % 2
                    ri += 1
                    S.dma("sp", hr[b], Hc[j, :, t0 + 1:t0 + 1 + TT], writes=[("hr", b)])
                    tt(ta[b2], ps[pa][:, :], rsa, ALU.mult, [PS(pa), ("rsa",)], [("ta", b2)])
                    tt(tb_[b2], ps[pr][:, :], rsr, ALU.mult, [PS(pr), ("rsr",)], [("tb", b2)])
                    tt(ta[b2], ta[b2], tb_[b2], ALU.add, [("ta", b2), ("tb", b2)], [("ta", b2)])
                    tt(hr[b], hr[b], ta[b2], ALU.add, [("hr", b), ("ta", b2)], [("hr", b)])
                    S.dma("pool", Hc[j, :, t0 + 1:t0 + 1 + TT], hr[b], reads=[("hr", b)], writes=[key("hst")])
            S.barrier()

            A.reset()
            NMC = NMEM // 128
            mn = A.alloc([DC, NMEM], BF16)
            kx = A.alloc([XC, NMEM], BF16)
            vx = A.alloc([NMC, XC * 128], BF16)
            vxT = [A.alloc([1, NMEM], BF16)[:, 0, :] for _ in range(2)]
            wr = WRing(3, DC)
            S.dma("sp", mn, memnTd.ap().rearrange("c p t -> p c t"), writes=[("mn",)])
            for j in range(2 * XC):
                wbuf, wk = wr.load("w_ckv", l, j, 0, DC)
                zb = j % 2
                for c in range(DC):
                    mm(ps[zb][:, 0:NMEM], wbuf[:, c, :], mn[:, c, :], c == 0, c == DC - 1, [wk, ("mn",)], [PS(zb)])
                if j < XC:
                    actf(kx[:, j, :], ps[zb][:, 0:NMEM], AF.Copy, [PS(zb)], [("kx", j)])
                else:
                    b = j % 2
                    actf(vxT[b], ps[zb][:, 0:NMEM], AF.Copy, [PS(zb)], [("vxT", b)])
                    psb = ps[5][:].bitcast(BF16)
                    for s in range(NMC):
                        S.add("pe", (lambda o, i: lambda e: e.transpose(out=o, in_=i, identity=ident))(
                            psb[:, s * 128:(s + 1) * 128], vxT[b][:, s * 128:(s + 1) * 128]), [("vxT", b)], [PS(5)])
                    S.add("dve", (lambda o, i: lambda e: e.tensor_copy(out=o, in_=i))(
                        vx[:, :, (j - XC) * 128:(j - XC + 1) * 128], psb[:, 0:NMC * 128].rearrange("p (s d) -> p s d", s=NMC)),
                        [PS(5)], [("vx", j - XC)])
            nrm = make_nrm(4, 512, [6, 7])
            hn = A.alloc([DC, 512], BF16)
            qx = A.alloc([XC, 512], BF16)
            ox = A.alloc([XC, 512], BF16)
            ptx = [A.alloc([1, 512], BF16)[:, 0, :] for _ in range(4)]
            rlx = A.alloc([1, 512], F32)[:, 0, :]
            hr = [A.alloc([1, 512], F32)[:, 0, :] for _ in range(3)]
            wr2 = WRing(3, XC)
            ri = 0
            pti = 0
            scx = 256.0 ** -0.5
            for tI in range(NT):
                t0 = tI * TT
                norm_tile(Hc, [], t0 + 1, t0 + 1 + TT, lambda cg: pcol("nc", cg), hn, 0, nrm)
                for j in range(XC):
                    wbuf, wk = wr.load("w_cq", l, j, 0, DC)
                    zb = j % 2
                    for c in range(DC):
                        mm(ps[zb][:, :], wbuf[:, c, :], hn[:, c, :], c == 0, c == DC - 1, [wk, ("hn", c)], [PS(zb)])
                    actf(qx[:, j, :], ps[zb][:, :], AF.Copy, [PS(zb)], [("qx", j)])
                for xh in range(XC // 2):
                    pts = []
                    for mc in range(NMC):
                        pb = 2 + (mc % 2)
                        for e_ in range(2):
                            cch = xh * 2 + e_
                            mm(ps[pb][:, :], kx[:, cch, mc * 128:(mc + 1) * 128], qx[:, cch, :], e_ == 0, e_ == 1,
                               [("kx", cch), ("qx", cch)], [PS(pb)])
                        p_ = pti % 4
                        pti += 1
                        actf(ptx[p_], ps[pb][:, :], AF.Exp, [PS(pb)], [("ptx", p_)], scale=scx)
                        pts.append(p_)
                    for mc in range(NMC):
                        mm(ps[4][:, :], ones, ptx[pts[mc]], mc == 0, mc == NMC - 1, [("ptx", pts[mc])], [PS(4)])
                    recip(rlx, ps[4][:, :], [PS(4)], [("rlx",)])
                    for e_ in range(2):
                        cch = xh * 2 + e_
                        pb = 6 + e_
                        for mc in range(NMC):
                            mm(ps[pb][:, :], vx[:, mc, cch * 128:(cch + 1) * 128], ptx[pts[mc]], mc == 0, mc == NMC - 1,
                               [("vx", cch), ("ptx", pts[mc])], [PS(pb)])
                        tt(ox[:, cch, :], ps[pb][:, :], rlx, ALU.mult, [PS(pb), ("rlx",)], [("ox", cch)])
                for j in range(DC):
                    wbuf, wk = wr2.load("w_co", l, j, 0, XC)
                    zb = j % 2
                    for c in range(XC):
                        mm(ps[zb][:, :], wbuf[:, c, :], ox[:, c, :], c == 0, c == XC - 1, [wk, ("ox", c)], [PS(zb)])
                    b = ri % 3
                    ri += 1
                    S.dma("sp", hr[b], Hc[j, :, t0 + 1:t0 + 1 + TT], writes=[("hr", b)])
                    tt(hr[b], hr[b], ps[zb][:, :], ALU.add, [("hr", b), PS(zb)], [("hr", b)])
                    S.dma("pool", Hc[j, :, t0 + 1:t0 + 1 + TT], hr[b], reads=[("hr", b)], writes=[key("hst")])
                    if tI == 0:
                        S.dma("pool", exHv[j, :, 0:1], hr[b][:, 0:1], reads=[("hr", b)], writes=[key("exh")], slow=True)
                    if tI == NT - 1:
                        S.dma("pool", exHv[j, :, 1:2], hr[b][:, TT - 1:TT], reads=[("hr", b)], writes=[key("exh")], slow=True)
            S.barrier()
            pair_gather(exH.ap(), gH.ap(), ("exH",), ("gH",))
            for c0 in range(0, DC, 8):
                S.dma("sp", Hc.ap()[c0:c0 + 8, :, 0:1], gHv[0][c0:c0 + 8, :, 1:2], reads=[("gH",)], writes=[key("hh")], slow=True)
                S.dma("sp", Hc.ap()[c0:c0 + 8, :, TH + 1:TH + 2], gHv[1][c0:c0 + 8, :, 0:1], reads=[("gH",)], writes=[key("hh")], slow=True)
            S.barrier()

            A.reset()
            nrm = make_nrm(4, 512, [6, 7])
            hn = A.alloc([DC, 512], BF16)
            wrg = WRing(4, DC)
            cg_ = [A.alloc([1, 512], F32)[:, 0, :] for _ in range(2)]
            cv_ = [A.alloc([1, 512], F32)[:, 0, :] for _ in range(2)]
            aob = [A.alloc([1, 512], BF16)[:, 0, :] for _ in range(3)]
            ci = 0
            fw, fb = poff["fw"], poff["fb"]
            s0 = 0
            while s0 < TH:
                nout = min(510, TH - s0)
                ncol = nout + 2
                norm_tile(Hc, [], s0, s0 + ncol, lambda cg: pcol("nf", cg), hn, 0, nrm)
                hk = [("hn", c) for c in range(DC)]
                if s0 == 0:
                    ts(hn[:, :, 0:1], hn[:, :, 0:1], msk_sb[:, 0:1], None, ALU.mult, ALU.bypass, hk, hk)
                if s0 + nout == TH:
                    ts(hn[:, :, ncol - 1:ncol], hn[:, :, ncol - 1:ncol], msk_sb[:, 1:2], None, ALU.mult, ALU.bypass, hk, hk)
                for j in range(FC):
                    wg, wgk = wrg.load("w_up", l, j, 0, DC)
                    wv, wvk = wrg.load("w_up", l, FC + j, 0, DC)
                    pg, pv = (j % 2) * 2, (j % 2) * 2 + 1
                    for c in range(DC):
                        mm(ps[pg][:, 0:ncol], wg[:, c, :], hn[:, c, 0:ncol], c == 0, c == DC - 1, [wgk, ("hn", c)], [PS(pg)])
                    for c in range(DC):
                        mm(ps[pv][:, 0:ncol], wv[:, c, :], hn[:, c, 0:ncol], c == 0, c == DC - 1, [wvk, ("hn", c)], [PS(pv)])
                    b = ci % 2
                    b3 = ci % 3
                    ci += 1
                    for (pp_, cb_, jj, kk) in ((pg, cg_[b], j, ("cg", b)), (pv, cv_[b], FC + j, ("cv", b))):
                        actf(cb_[:, 0:nout], ps[pp_][:, 0:nout], AF.Identity, [PS(pp_), ("ppl",)], [kk],
                             scale=ppl_sb[:, fw + jj:fw + jj + 1], bias=ppl_sb[:, fb + jj:fb + jj + 1])
                        for k in (1, 2):
                            stt(cb_[:, 0:nout], ps[pp_][:, k:k + nout], ppl_sb[:, fw + k * 2 * FC + jj:fw + k * 2 * FC + jj + 1],
                                cb_[:, 0:nout], ALU.mult, ALU.add, [PS(pp_), kk, ("ppl",)], [kk])
                    actf(cg_[b][:, 0:nout], cg_[b][:, 0:nout], AF.Gelu_apprx_tanh, [("cg", b)], [("cg", b)])
                    tt(aob[b3][:, 0:nout], cg_[b][:, 0:nout], cv_[b][:, 0:nout], ALU.mult, [("cg", b), ("cv", b)], [("aob", b3)])
                    S.dma("pool", actd[j, :, s0:s0 + nout], aob[b3][:, 0:nout], reads=[("aob", b3)], writes=[key("actd")])
                s0 += nout
            S.barrier()

            A.reset()
            actT = A.alloc([FC, 512], BF16)
            wrd = WRing(4, 32)
            hr = [A.alloc([1, 512], F32)[:, 0, :] for _ in range(3)]
            ri = 0
            for tI in range(NT):
                t0 = tI * TT
                for c0 in range(0, FC, 24):
                    c1 = min(c0 + 24, FC)
                    S.dma("sp", actT[:, c0:c1, :], actd.ap()[c0:c1, :, t0:t0 + TT].rearrange("c p t -> p c t"),
                          writes=[("actT", c0 // 24)])
                for j in range(DC):
                    pd = 4 + (j % 2)
                    for c0 in range(0, FC, 32):
                        c1 = min(c0 + 32, FC)
                        wd, wdk = wrd.load("w_down", l, j, c0, c1)
                        for c in range(c0, c1):
                            mm(ps[pd][:, :], wd[:, c - c0, :], actT[:, c, :], c == 0, c == FC - 1,
                               [wdk, ("actT", c // 24)], [PS(pd)])
                    b = ri % 3
                    ri += 1
                    S.dma("sp", hr[b], Hc[j, :, t0 + 1:t0 + 1 + TT], writes=[("hr", b)])
                    tt(hr[b], hr[b], ps[pd][:, :], ALU.add, [("hr", b), PS(pd)], [("hr", b)])
                    S.dma("pool", Hn[j, :, t0 + 1:t0 + 1 + TT], hr[b], reads=[("hr", b)], writes=[key("hst")])
            S.barrier()

        A.reset()
        Hf = H[L % 2]
        nrm = make_nrm(4, 512, [6, 7])
        ob = [A.alloc([4, 512], F32) for _ in range(2)]
        oi = [0]
        for tI in range(NT):
            t0 = tI * TT

            def outf(gi, c, b, raw, rstd, t0=t0):
                if c == 0:
                    oi[0] += 1
                o = ob[oi[0] % 2]
                cg = gi * 4 + c
                stt(o[:, c, :], raw[:, c, 0:TT], ppg_sb[:, DC + cg:DC + cg + 1], rstd[:, 0:TT], ALU.mult, ALU.mult,
                    [("raw", b), ("rstd",)], [("ob", oi[0] % 2, c)])
                if c == 3:
                    S.dma("pool", outT.ap()[gi * 4:(gi + 1) * 4, :, t0:t0 + TT].rearrange("c p t -> p c t"), o,
                          reads=[("ob", oi[0] % 2, cc_) for cc_ in range(4)], writes=[key("out")])
            norm_tile(Hf, [], t0 + 1, t0 + 1 + TT, None, None, 0, nrm, out_f32=outf)
        S.barrier()
        for e in ("sp",):
            S.add(e, lambda h: h.nop(), [], [], kind="nopx")

        S.finalize()
        with nc.Block() as block:
            @block.tensor
            def _(h):
                S.emit("pe", h)

            @block.scalar
            def _(h):
                S.emit("act", h)

            @block.vector
            def _(h):
                S.emit("dve", h)

            @block.gpsimd
            def _(h):
                S.emit("pool", h)

            @block.sync
            def _(h):
                S.emit("sp", h)
    return nc


def _blocked(W):
    K, Fd = W.shape
    return np.ascontiguousarray(W.reshape(K // 128, 128, Fd // 128, 128).transpose(2, 1, 0, 3)).reshape(Fd, K)


def _fm(v):
    return np.ascontiguousarray(v.reshape(-1, 128).T)


def rope_tables(T, GRID_W):
    rows = T // GRID_W
    row = np.repeat(np.arange(rows, dtype=np.float32), GRID_W)
    col = np.tile(np.arange(GRID_W, dtype=np.float32), rows)
    inv = (np.float32(10000.0) ** (-np.arange(0, 64, 2, dtype=np.float32) / np.float32(64))).astype(np.float32)
    ang = np.stack([row[:, None] * inv, col[:, None] * inv], axis=1).astype(np.float32)
    cos, sin = np.cos(ang).astype(np.float32), np.sin(ang).astype(np.float32)
    C = np.zeros((128, T), np.float32)
    Sn = np.zeros((128, T), np.float32)
    for d in range(128):
        ax, i = d // 64, d % 32
        C[d] = cos[:, ax, i]
        Sn[d] = sin[:, ax, i]
    return C, Sn


def const_mats():
    ones = np.ones((128, 128), np.float32)
    ident = np.eye(128, dtype=np.float32)
    P = np.zeros((128, 128), np.float32)
    for d in range(128):
        if (d // 32) % 2 == 0:
            P[d, d + 32] = -1.0
        else:
            P[d, d - 32] = 1.0
    return np.concatenate([ones, ident, P.T], axis=1).astype(ml_dtypes.bfloat16)


def prepare_inputs(cfg, inp):
    L, DC, RC, FC, T = cfg["L"], cfg["DC"], cfg["RC"], cfg["FC"], cfg["T"]
    poff, NPL = cfg["poff"], cfg["NPL"]
    ppl = np.zeros((L, 128, NPL), np.float32)
    for l in range(L):
        def put(name, arr):
            ppl[l][:, poff[name]:poff[name] + arr.shape[1]] = arr
        put("nm", _fm(inp["norm_mix"][l]))
        put("nc", _fm(inp["norm_cross"][l]))
        put("nf", _fm(inp["norm_ffn"][l]))
        put("gn", _fm(inp["group_norm"][l]))
        put("qn", inp["q_norm"][l].reshape(128, 1))
        put("kn", inp["k_norm"][l].reshape(128, 1))
        put("cw", np.concatenate([_fm(inp["rg_conv_w"][l][k]) for k in range(4)], axis=1))
        put("cb", _fm(inp["rg_conv_b"][l]))
        put("gb", np.concatenate([_fm(inp["rg_gate_b"][l][d][g]) for d in range(2) for g in range(2)], axis=1))
        put("lam", np.concatenate([_fm(inp["rg_lambda"][l][d]) for d in range(2)], axis=1))
        put("fw", np.concatenate([_fm(inp["ffn_conv_w"][l][k]) for k in range(3)], axis=1))
        put("fb", _fm(inp["ffn_conv_b"][l]))
    ppg = np.concatenate([_fm(inp["mem_norm"]), _fm(inp["final_norm"])], axis=1).astype(np.float32)
    C, Sn = rope_tables(T, cfg["GRID_W"])
    TH = cfg["TH"]
    cm = const_mats()
    wsh = {}
    for name, (rows, cols) in cfg["W"].items():
        wsh[name] = np.empty((NCORES, L, 128, rows * cols // NCORES // 128), np.float32)
    for l in range(L):
        for base in ("w_in", "gate", "w_out", "w_ckv", "w_cq", "w_co", "w_up", "w_down"):
            if base == "gate":
                g = inp["rg_gate_w"][l]
                Wb = np.ascontiguousarray(g.transpose(2, 3, 0, 1, 4)).reshape(RC * 128, 4 * 128)
            else:
                Wb = _blocked(inp[base][l])
            if base in cfg["SPLIT"]:
                for q in range(cfg["SPLIT"][base]):
                    name = base + str(q)
                    rows = cfg["W"][name][0]
                    wsh[name][:, l] = Wb[q * rows:(q + 1) * rows].reshape(NCORES, 128, -1)
            else:
                wsh[base][:, l] = Wb.reshape(NCORES, 128, -1)
            del Wb
    in_maps = []
    for c in range(NCORES):
        b, p = c // 2, c % 2
        mk = np.zeros((128, 2), np.float32)
        mk[:, 0] = p
        mk[:, 1] = 1 - p
        m = {
            "xT": np.ascontiguousarray(inp["x"][b, p * TH:(p + 1) * TH].T).reshape(DC, 128, TH),
            "memT": np.ascontiguousarray(inp["mem"][b].T).reshape(DC, 128, cfg["NMEM"]),
            "ppl": ppl, "ppg": ppg, "cosT": np.ascontiguousarray(C[:, p * TH:(p + 1) * TH]),
            "sinT": np.ascontiguousarray(Sn[:, p * TH:(p + 1) * TH]), "cmat": cm, "msk": mk,
        }
        for name in cfg["W"]:
            m["e_" + name] = wsh[name][c]
        in_maps.append(m)
    return in_maps


_CACHE = {}


def run(cfg, inp):
    key_ = (cfg["D"], cfg["T"], cfg["L"])
    if key_ not in _CACHE:
        _CACHE[key_] = build_program(cfg)
    nc = _CACHE[key_]
    in_maps = prepare_inputs(cfg, inp)
    res = run_bass_kernel_spmd(nc, in_maps, core_ids=list(range(NCORES)))
    B, D, T = cfg["B"], cfg["D"], cfg["T"]
    out = np.empty((B, T, D), np.float32)
    TH = cfg["TH"]
    for c in range(NCORES):
        b, p = c // 2, c % 2
        out[b, p * TH:(p + 1) * TH] = res.results[c]["outT"].reshape(D, TH).T
    return out


def kernel(**inputs):
    inp = {k: np.asarray(v) for k, v in inputs.items()}
    cfg = make_cfg()
    return run(cfg, inp)
```
